# Optimizing a Trainium2 kernel written in Bass

```python
import jax, jax.numpy as jnp
from jax import lax
import numpy as np

D_MODEL = 2048
BATCH = 1
SEQ = 8192
DEPTH = 4

N_META = 16
N_HEADS = 16
HEAD_DIM = 128
D_ATTN = N_HEADS * HEAD_DIM
KV_RANK = 256
IDX_HEADS = 8
IDX_DIM = 64
TOPK_MAX = 256
POOL_WINDOWS = (2, 4, 8, 16)
N_POOL_GROUPS = 4
D_POOL = D_MODEL // 2
POOL_GROUP = D_POOL // N_POOL_GROUPS
D_FF = 4 * D_MODEL
Q_BLOCK = 128
EPS = 1e-6
SPLIT_SIZES = (D_ATTN, KV_RANK, IDX_HEADS * IDX_DIM, IDX_DIM, IDX_HEADS, D_POOL, D_MODEL, D_MODEL)
N_IN = sum(SPLIT_SIZES)

kernel_name = "hybrid_dsa_multiscale_pool_gated"


def rms_norm(x, g):
    xf = x.astype(jnp.float32)
    y = xf * lax.rsqrt(jnp.mean(xf * xf, axis=-1, keepdims=True) + EPS)
    return (y * g.astype(jnp.float32)).astype(x.dtype)


def dsa_attention(q, c_kv, q_idx, k_idx, w_idx, w_uk, w_uv, n_keys):
    B, TP = q.shape[0], q.shape[1]
    topk = min(TOPK_MAX, n_keys // 4)
    nb = TP // Q_BLOCK
    q_lat = jnp.einsum('bthe,hce->bthc', q, w_uk)
    key_pos = jnp.arange(TP)
    batch_ix = jnp.arange(B)[:, None, None]

    def to_blocks(a):
        return a.reshape((B, nb, Q_BLOCK) + a.shape[2:]).swapaxes(0, 1)

    def block(args):
        ql, qi, wi, start = args
        qpos = start + jnp.arange(Q_BLOCK)
        causal = key_pos[None, :] <= qpos[:, None]
        logits = jnp.einsum('bqhd,bsd->bqhs', qi, k_idx).astype(jnp.float32) * (IDX_DIM ** -0.5)
        score = jnp.einsum('bqh,bqhs->bqs', wi.astype(jnp.float32), jax.nn.relu(logits))
        score = jnp.where(causal[None], score, -jnp.inf)
        _, idx = lax.top_k(score, topk)
        sel = c_kv[batch_ix, idx]
        s = jnp.einsum('bqhr,bqkr->bqhk', ql, sel).astype(jnp.float32) * (HEAD_DIM ** -0.5)
        valid = idx <= qpos[None, :, None]
        s = jnp.where(valid[:, :, None, :], s, -jnp.inf)
        p = jax.nn.softmax(s, axis=-1).astype(sel.dtype)
        return jnp.einsum('bqhk,bqkr->bqhr', p, sel)

    starts = jnp.arange(nb) * Q_BLOCK
    o_lat = lax.map(block, (to_blocks(q_lat), to_blocks(q_idx), to_blocks(w_idx), starts))
    o_lat = o_lat.swapaxes(0, 1).reshape(B, TP, N_HEADS, KV_RANK)
    o = jnp.einsum('bthc,hce->bthe', o_lat, w_uv)
    return o.reshape(B, TP, D_ATTN)


def multiscale_pool(p, w_pool, scale):
    B, TP = p.shape[0], p.shape[1]
    pg = p.astype(jnp.float32).reshape(B, TP, N_POOL_GROUPS, POOL_GROUP)
    csum = jnp.concatenate([jnp.zeros_like(pg[:, :1]), jnp.cumsum(pg, axis=1)], axis=1)
    t1 = jnp.arange(1, TP + 1)[:, None]
    win = jnp.array(POOL_WINDOWS, dtype=jnp.int32)[None, :]
    lo = jnp.maximum(t1 - win, 0)
    gix = jnp.arange(N_POOL_GROUPS)[None, :]
    window_sum = csum[:, 1:] - csum[:, lo, gix]
    count = (t1 - lo).astype(jnp.float32)[None, :, :, None]
    y = (window_sum / count - pg).astype(p.dtype)
    y = jnp.einsum('btgc,gcd->btgd', y, w_pool).reshape(B, TP, D_POOL)
    return y * scale


def setup_inputs(seed: int = 0) -> dict:
    key = jax.random.key(seed)
    ks = jax.random.split(key, 17)
    f32 = jnp.float32

    def nrm(k, shape, fan_in):
        return jax.random.normal(k, shape, f32) * (fan_in ** -0.5)

    def gain(k, shape):
        return 1.0 + 0.02 * jax.random.normal(k, shape, f32)

    return {
        "x": jax.random.normal(ks[0], (BATCH, SEQ, D_MODEL), f32),
        "meta_tokens": jax.random.normal(ks[1], (N_META, D_MODEL), f32),
        "norm_mix_g": gain(ks[2], (DEPTH, D_MODEL)),
        "w_in": nrm(ks[3], (DEPTH, D_MODEL, N_IN), D_MODEL),
        "kv_norm_g": gain(ks[4], (DEPTH, KV_RANK)),
        "idx_k_norm_g": gain(ks[5], (DEPTH, IDX_DIM)),
        "w_uk": nrm(ks[6], (DEPTH, N_HEADS, KV_RANK, HEAD_DIM), KV_RANK),
        "w_uv": nrm(ks[7], (DEPTH, N_HEADS, KV_RANK, HEAD_DIM), KV_RANK),
        "w_attn_o": nrm(ks[8], (DEPTH, D_ATTN, D_MODEL), D_ATTN),
        "w_pool": nrm(ks[9], (DEPTH, N_POOL_GROUPS, POOL_GROUP, POOL_GROUP), POOL_GROUP),
        "pool_scale": gain(ks[10], (DEPTH, D_POOL)),
        "w_pool_o": nrm(ks[11], (DEPTH, D_POOL, D_MODEL), D_POOL),
        "w_out": nrm(ks[12], (DEPTH, D_MODEL, D_MODEL), D_MODEL),
        "norm_mlp_g": gain(ks[13], (DEPTH, D_MODEL)),
        "w_mlp_in": nrm(ks[14], (DEPTH, D_MODEL, D_FF), D_MODEL),
        "w_mlp_out": nrm(ks[15], (DEPTH, D_FF, D_MODEL), D_FF),
        "final_norm_g": gain(ks[16], (D_MODEL,)),
    }


def reference(x, meta_tokens, norm_mix_g, w_in, kv_norm_g, idx_k_norm_g, w_uk, w_uv, w_attn_o,
              w_pool, pool_scale, w_pool_o, w_out, norm_mlp_g, w_mlp_in, w_mlp_out, final_norm_g):
    B, S, D = x.shape
    n_keys = S + N_META
    tp = -(-n_keys // Q_BLOCK) * Q_BLOCK
    meta = jnp.broadcast_to(meta_tokens.astype(x.dtype)[None], (B, N_META, D))
    pad = jnp.zeros((B, tp - n_keys, D), x.dtype)
    h_res = jnp.concatenate([meta, x, pad], axis=1)
    split_points = [int(v) for v in np.cumsum(SPLIT_SIZES)[:-1]]

    for l in range(DEPTH):
        h = rms_norm(h_res, norm_mix_g[l])
        proj = h @ w_in[l]
        q, c_kv, q_idx, k_idx, w_idx, p_in, g_a, g_b = jnp.split(proj, split_points, axis=-1)
        q = q.reshape(B, tp, N_HEADS, HEAD_DIM)
        c_kv = rms_norm(c_kv, kv_norm_g[l])
        q_idx = q_idx.reshape(B, tp, IDX_HEADS, IDX_DIM)
        k_idx = rms_norm(k_idx, idx_k_norm_g[l])
        w_idx = w_idx * (IDX_HEADS ** -0.5)
        a = dsa_attention(q, c_kv, q_idx, k_idx, w_idx, w_uk[l], w_uv[l], n_keys) @ w_attn_o[l]
        b = multiscale_pool(p_in, w_pool[l], pool_scale[l]) @ w_pool_o[l]
        merged = jax.nn.sigmoid(g_a) * a + jax.nn.sigmoid(g_b) * b
        h_res = h_res + merged @ w_out[l]
        h2 = rms_norm(h_res, norm_mlp_g[l])
        h_res = h_res + jnp.square(jax.nn.relu(h2 @ w_mlp_in[l])) @ w_mlp_out[l]

    y = rms_norm(h_res, final_norm_g)
    return y[:, N_META:N_META + S]
```

```python
import numpy as np
from contextlib import ExitStack
import concourse.bass as bass
import concourse.mybir as mybir
from concourse.bass_utils import run_bass_kernel_spmd

F32 = mybir.dt.float32
BF16 = mybir.dt.bfloat16
AF = mybir.ActivationFunctionType
ALU = mybir.AluOpType
AX = mybir.AxisListType

NCORE = 8
D = 2048
NT = 1040
NX = 1024
NM = 16
NKEY = 8208
DEPTH = 4
EPS = 1e-6
NEG = -1.0e30
TOPK = 256
NBIS = 18
SEM_LIMIT = 30000
COLS3 = [(0, 352), (352, 696), (696, 1040)]
KC = 16

QB = [(128 * b, 128) for b in range(16)]
CKVB = [(2048 + 128 * b, 128) for b in range(2)]
QIB = [(2304 + 128 * b, 128) for b in range(4)]
PINB = [(2888 + 128 * b, 128) for b in range(8)]
GAB = [(3912 + 128 * b, 128) for b in range(16)]
GBB = [(5960 + 128 * b, 128) for b in range(16)]
WIN_BLOCKS = QB + CKVB + QIB + PINB + GAB + GBB
KI_COL = 2816
WI_COL = 2880


class Buf:
    __slots__ = ("name", "w", "r", "dsem", "dcount")

    def __init__(self, name):
        self.name = name
        self.w = None
        self.r = []
        self.dsem = None
        self.dcount = 0


class Sched:
    ENGS = ("pe", "act", "dve", "pool", "sp")

    def __init__(self, nc, ctx, same_engine_sync=True):
        self.nc = nc
        self.ctx = ctx
        self.same_engine_sync = same_engine_sync
        self.streams = {e: [] for e in self.ENGS}
        self.cur_sem = {}
        self.cur_cnt = {}
        self.seen = {e: {} for e in self.ENGS}
        self.nsem = 0
        self.sem_owner = {}
        self.latest = {}
        for e in self.ENGS:
            self._roll(e)
        self.ninstr = {e: 0 for e in self.ENGS}

    def new_sem(self, tag="s"):
        self.nsem += 1
        return self.ctx.enter_context(self.nc.semaphore(f"{tag}{self.nsem}"))

    def _roll(self, e):
        s = self.new_sem(e)
        self.cur_sem[e] = s
        self.cur_cnt[e] = 0
        self.sem_owner[id(s)] = e

    def buf(self, name):
        return Buf(name)

    def _collect(self, eng, reads, writes):
        need = {}

        def add(ev):
            if ev is None:
                return
            s, v = ev
            k = id(s)
            if k not in need or need[k][1] < v:
                need[k] = (s, v)

        for b in reads:
            add(b.w)
        for b in writes:
            add(b.w)
            for ev in b.r:
                add(ev)
        out = []
        seen = self.seen[eng]
        for k, (s, v) in need.items():
            if self.sem_owner.get(k) == eng:
                if eng == "pe" or not self.same_engine_sync:
                    continue
            if seen.get(k, -1) >= v:
                continue
            seen[k] = v
            out.append((s, v))
        return out

    def _update(self, ev, reads, writes):
        s, v = ev
        self.latest[id(s)] = (s, v)
        for b in reads:
            b.r.append(ev)
            if len(b.r) > 32:
                d = {}
                for s2, v2 in b.r:
                    k = id(s2)
                    if k not in d or d[k][1] < v2:
                        d[k] = (s2, v2)
                b.r = list(d.values())
        for b in writes:
            b.w = ev
            b.r = []

    def op(self, eng, fn, reads=(), writes=(), **kw):
        if isinstance(fn, str):
            fn = (fn, kw)
        if self.cur_cnt[eng] >= SEM_LIMIT:
            self._roll(eng)
        waits = self._collect(eng, reads, writes)
        sem = self.cur_sem[eng]
        self.cur_cnt[eng] += 1
        ev = (sem, self.cur_cnt[eng])
        self._update(ev, reads, writes)
        self.streams[eng].append((waits, fn, sem, 1))
        self.ninstr[eng] += 1
        return ev

    def dma(self, eng, fn=None, reads=(), writes=(), tgt=None, **kw):
        if fn is None:
            fn = ("dma_start", kw)
        if tgt is None:
            tgt = writes[0]
        if tgt.dsem is None:
            tgt.dsem = self.new_sem("d")
        waits = self._collect(eng, reads, writes)
        tgt.dcount += 16
        ev = (tgt.dsem, tgt.dcount)
        self._update(ev, reads, writes)
        self.streams[eng].append((waits, fn, tgt.dsem, 16))
        self.ninstr[eng] += 1
        return ev

    def barrier(self, engs=None):
        for e in (engs or self.ENGS):
            waits = []
            seen = self.seen[e]
            for k, (s, v) in self.latest.items():
                if self.sem_owner.get(k) == e:
                    continue
                if seen.get(k, -1) >= v:
                    continue
                seen[k] = v
                waits.append((s, v))
            if waits:
                self.streams[e].append((waits, None, None, 0))

    def emit(self):
        nc = self.nc
        engmap = {"pe": "tensor", "act": "scalar", "dve": "vector", "pool": "gpsimd", "sp": "sync"}
        with nc.Block() as block:
            for e in self.ENGS:
                stream = self.streams[e]
                if not stream:
                    continue

                def body(engine, stream=stream):
                    for waits, fn, sem, inc in stream:
                        for s, v in waits:
                            engine.wait_ge(s, v)
                        if fn is not None:
                            if isinstance(fn, tuple):
                                getattr(engine, fn[0])(**fn[1]).then_inc(sem, inc)
                            else:
                                fn(engine).then_inc(sem, inc)

                getattr(block, engmap[e])(body)


class KB:
    def __init__(self):
        self.nc = bass.Bass("TRN2", target_bir_lowering=False)
        self.ctx = ExitStack()
        self.S = Sched(self.nc, self.ctx)
        self.dr = {}
        nc, S = self.nc, self.S
        self.ps = []
        self.psb = []
        for i in range(8):
            self.ps.append(self.ctx.enter_context(nc.psum_tensor(f"ps{i}", [128, 512], F32)))
            self.psb.append(S.buf(f"ps{i}"))
        self.ones = self.sb("ones", [128, 128], BF16)
        self.ident = self.sb("ident", [128, 128], BF16)
        self.idf = self.sb("idf", [128, 128], F32)
        o, ob = self.ones
        S.op("dve", "memset", [], [ob], ap=o[:], constant=1.0)
        idf, idfb = self.idf
        idn, idnb = self.ident
        S.op("pool", "iota", [], [idfb], out=idf[:], pattern=[[1, 128]], base=0, channel_multiplier=-1,
             allow_small_or_imprecise_dtypes=True)
        S.op("dve", "tensor_single_scalar", [idfb], [idnb], out=idn[:], in_=idf[:], scalar=0.0, op=ALU.is_equal)
        self.rr = 0

    def sb(self, name, shape, dtype, ctx=None):
        t = (ctx or self.ctx).enter_context(self.nc.sbuf_tensor(name, shape, dtype))
        return t, self.S.buf(name)

    def dram(self, name, shape, dtype, kind):
        if name in self.dr:
            return self.dr[name]
        k = {"in": "ExternalInput", "out": "ExternalOutput", "tmp": "Internal"}[kind]
        t = self.nc.dram_tensor(name, list(shape), dtype, kind=k)
        self.dr[name] = (t.ap(), self.S.buf(name))
        return self.dr[name]

    def evac(self, out, in_, reads, writes, eng=None):
        S = self.S
        if eng is None:
            eng = "act" if self.rr % 2 == 0 else "dve"
            self.rr += 1
        if eng == "act":
            S.op("act", "copy", reads, writes, out=out, in_=in_)
        else:
            S.op("dve", "tensor_copy", reads, writes, out=out, in_=in_)

    def gemm(self, wblocks, epilogue, slots, banksets, cols=COLS3):
        S = self.S
        nsl = len(slots)
        nb = len(wblocks)

        def load(bi):
            wap, wb, w, act, actb, kc_n = wblocks[bi]
            sl, slb = slots[bi % nsl]
            for k0 in range(0, kc_n, 8):
                k1 = min(kc_n, k0 + 8)
                S.dma("pool", None, [wb], [slb], out=sl[:, k0:k1, :w], in_=wap[:, k0:k1, :])

        for bi in range(min(nsl - 1, nb)):
            load(bi)
        for bi in range(nb):
            if bi + nsl - 1 < nb:
                load(bi + nsl - 1)
            wap, wb, w, act, actb, kc_n = wblocks[bi]
            sl, slb = slots[bi % nsl]
            bset = banksets[bi % len(banksets)]
            banks = []
            for ci, (c0, c1) in enumerate(cols):
                banks.append((self.ps[bset[ci]], self.psb[bset[ci]], c0, c1))
            for kc in range(kc_n):
                for (p, pb, c0, c1) in banks:
                    S.op("pe", "matmul", [slb, actb], [pb], out=p[:w, :c1 - c0], lhsT=sl[:, kc, :w],
                         rhs=act[:, kc, c0:c1], start=(kc == 0), stop=(kc == kc_n - 1))
            epilogue(bi, banks)

    def rms_rstd(self, src, srcb, nch, npart, nfeat, sq, sqb, rstd, rstdb, banks):
        S = self.S
        ones, onesb = self.ones
        for ch in range(nch):
            S.op("act", "activation", [srcb], [sqb], out=sq[:npart, ch, :], in_=src[:npart, ch, :], func=AF.Square)
        for ci, (c0, c1) in enumerate(COLS3):
            p, pb = self.ps[banks[ci]], self.psb[banks[ci]]
            for ch in range(nch):
                S.op("pe", "matmul", [onesb, sqb], [pb], out=p[:, :c1 - c0], lhsT=ones[:npart, :],
                     rhs=sq[:npart, ch, c0:c1], start=(ch == 0), stop=(ch == nch - 1))
            S.op("dve", "tensor_scalar", [pb], [rstdb], out=rstd[:, c0:c1], in0=p[:, :c1 - c0],
                 scalar1=1.0 / nfeat, scalar2=EPS, op0=ALU.mult, op1=ALU.add)
        S.op("act", "activation", [rstdb], [rstdb], out=rstd[:], in_=rstd[:], func=AF.Sqrt)
        S.op("dve", "reciprocal", [rstdb], [rstdb], out=rstd[:], in_=rstd[:])

    def phase_a(self, L, io):
        nc, S = self.nc, self.S
        xT, xTb = io("in", f"xT{L}", [D, NT], F32)
        wblk, wblkb = io("w", f"w_in_blk{L}", [62, 128, KC, 128], F32)
        wki, wkib = io("w", f"w_ki{L}", [128, KC, 64], F32)
        wwi, wwib = io("w", f"w_wi{L}", [128, KC, 8], F32)
        gmix, gmixb = io("w", f"g_mix{L}", [128, KC], F32)
        gkv, gkvb = io("w", f"g_kv{L}", [128, 2], F32)
        gki, gkib = io("w", f"g_ki{L}", [64, 1], F32)
        hT, hTb = io("out", f"hT{L}", [D, NT], BF16)
        qT, qTb = io("out", f"qT{L}", [D, NT], BF16)
        ckvnT, ckvnTb = io("out", f"ckvnT{L}", [256, NT], BF16)
        kinT, kinTb = io("out", f"kinT{L}", [64, NT], BF16)
        qiT, qiTb = io("out", f"qiT{L}", [512, NT], BF16)
        wtok, wtokb = io("out", f"wtok{L}", [9, 128, 8], F32)
        pinT, pinTb = io("out", f"pinT{L}", [1024, NT], F32)

        with ExitStack() as c:
            x32, x32b = self.sb("a_x32", [128, KC, NT], F32, c)
            hb, hbb = self.sb("a_hb", [128, KC, NT], BF16, c)
            rstd, rstdb = self.sb("a_rstd", [128, NT], F32, c)
            g1, g1b = self.sb("a_g1", [128, KC], F32, c)
            g2, g2b = self.sb("a_g2", [128, 2], F32, c)
            g3, g3b = self.sb("a_g3", [64, 1], F32, c)
            ck32, ck32b = self.sb("a_ck32", [128, 2, NT], F32, c)
            cksq, cksqb = self.sb("a_cksq", [128, 2, NT], BF16, c)
            rs2, rs2b = self.sb("a_rs2", [128, NT], F32, c)
            slots = [self.sb(f"a_ws{i}", [128, KC, 128], BF16, c) for i in range(3)]
            wkis, wkisb = self.sb("a_wki", [128, KC, 64], BF16, c)
            wwis, wwisb = self.sb("a_wwi", [128, KC, 8], BF16, c)
            stg = [self.sb(f"a_stg{i}", [128, NT], BF16, c) for i in range(4)]
            stf = [self.sb(f"a_stf{i}", [128, NT], F32, c) for i in range(2)]
            wts, wtsb = self.sb("a_wts", [128, 9, 8], F32, c)

            S.dma("sp", None, [gmixb], [g1b], out=g1[:], in_=gmix)
            S.dma("sp", None, [gkvb], [g2b], out=g2[:], in_=gkv)
            S.dma("sp", None, [gkib], [g3b], out=g3[:], in_=gki)
            xv = xT.rearrange("(kc p) n -> p kc n", p=128)
            for k0 in range(0, KC, 4):
                S.dma("sp", None, [xTb], [x32b], out=x32[:, k0:k0 + 4, :], in_=xv[:, k0:k0 + 4, :])
            S.dma("pool", None, [wkib], [wkisb], out=wkis[:], in_=wki)
            S.dma("pool", None, [wwib], [wwisb], out=wwis[:], in_=wwi)
            self.rms_rstd(x32, x32b, KC, 128, D, hb, hbb, rstd, rstdb, [0, 1, 2])
            for kc in range(KC):
                S.op("dve", "scalar_tensor_tensor", [x32b, g1b, rstdb], [hbb], out=hb[:, kc, :], in0=x32[:, kc, :],
                     scalar=g1[:, kc:kc + 1], in1=rstd[:], op0=ALU.mult, op1=ALU.mult)
            hv = hT.rearrange("(kc p) n -> p kc n", p=128)
            for k0 in range(0, KC, 4):
                S.dma("sp", None, [hbb], [hTb], out=hv[:, k0:k0 + 4, :], in_=hb[:, k0:k0 + 4, :])

            SC = (8.0 ** -0.5) * (64.0 ** -0.5)
            p7, p7b = self.ps[7], self.psb[7]
            for tj in range(9):
                c0 = tj * 128
                n = 128 if tj < 8 else NM
                for kc in range(KC):
                    S.op("pe", "matmul", [hbb, wwisb], [p7b], out=p7[:n, tj * 8:tj * 8 + 8],
                         lhsT=hb[:, kc, c0:c0 + n], rhs=wwis[:, kc, :], start=(kc == 0), stop=(kc == KC - 1))
            S.op("dve", "memset", [], [wtsb], ap=wts[:], constant=0.0)
            S.op("act", "mul", [p7b], [wtsb], out=wts[:, 0:8, :],
                 in_=p7[:, 0:64].rearrange("p (t h) -> p t h", h=8), mul=SC)
            S.op("act", "mul", [p7b], [wtsb], out=wts[:NM, 8, :], in_=p7[:NM, 64:72], mul=SC)
            S.dma("sp", None, [wtsb], [wtokb], out=wtok.rearrange("t p h -> p t h"), in_=wts[:])

            for kc in range(KC):
                for ci, (c0, c1) in enumerate(COLS3):
                    p, pb = self.ps[3 + ci], self.psb[3 + ci]
                    S.op("pe", "matmul", [wkisb, hbb], [pb], out=p[:64, :c1 - c0], lhsT=wkis[:, kc, :],
                         rhs=hb[:, kc, c0:c1], start=(kc == 0), stop=(kc == KC - 1))
            for ci, (c0, c1) in enumerate(COLS3):
                p, pb = self.ps[3 + ci], self.psb[3 + ci]
                S.op("act", "copy", [pb], [ck32b], out=ck32[:64, 0, c0:c1], in_=p[:64, :c1 - c0])
            self.rms_rstd(ck32, ck32b, 1, 64, 64, cksq, cksqb, rs2, rs2b, [3, 4, 5])
            st, stb = stg[3]
            S.op("dve", "scalar_tensor_tensor", [ck32b, g3b, rs2b], [stb], out=st[:64, :], in0=ck32[:64, 0, :],
                 scalar=g3[:, 0:1], in1=rs2[:64, :], op0=ALU.mult, op1=ALU.mult)
            S.dma("sp", None, [stb], [kinTb], out=kinT, in_=st[:64, :])

            blocks = [(wblk[bi], wblkb, 128, hb, hbb, KC) for bi in range(30)]
            cnt = {"bf": 0, "f": 0}

            def epi(bi, banks):
                if bi < 16 or 18 <= bi < 22:
                    st, stb = stg[cnt["bf"] % 3]
                    cnt["bf"] += 1
                    for (p, pb, c0, c1) in banks:
                        self.evac(st[:, c0:c1], p[:, :c1 - c0], [pb], [stb])
                    if bi < 16:
                        dst, dstb = qT[bi * 128:(bi + 1) * 128, :], qTb
                    else:
                        dst, dstb = qiT[(bi - 18) * 128:(bi - 17) * 128, :], qiTb
                    S.dma("sp", None, [stb], [dstb], out=dst, in_=st[:])
                elif bi < 18:
                    ch = bi - 16
                    for (p, pb, c0, c1) in banks:
                        self.evac(ck32[:, ch, c0:c1], p[:, :c1 - c0], [pb], [ck32b])
                else:
                    st, stb = stf[cnt["f"] % 2]
                    cnt["f"] += 1
                    for (p, pb, c0, c1) in banks:
                        self.evac(st[:, c0:c1], p[:, :c1 - c0], [pb], [stb])
                    r0 = (bi - 22) * 128
                    S.dma("sp", None, [stb], [pinTb], out=pinT[r0:r0 + 128, :], in_=st[:])

            self.gemm(blocks, epi, slots, [[0, 1, 2], [3, 4, 5]])

            self.rms_rstd(ck32, ck32b, 2, 128, 256, cksq, cksqb, rs2, rs2b, [0, 1, 2])
            for ch in range(2):
                st, stb = stg[ch]
                S.op("dve", "scalar_tensor_tensor", [ck32b, g2b, rs2b], [stb], out=st[:], in0=ck32[:, ch, :],
                     scalar=g2[:, ch:ch + 1], in1=rs2[:], op0=ALU.mult, op1=ALU.mult)
                S.dma("sp", None, [stb], [ckvnTb], out=ckvnT[ch * 128:(ch + 1) * 128, :], in_=st[:])
            S.barrier()

    def phase_b(self, L, io, nheads=16, ntiles=8, dbg=None):
        nc, S = self.nc, self.S
        kinA, kinAb = io("in", f"kinA{L}", [64, NKEY], BF16)
        ckvA, ckvAb = io("in", f"ckvA{L}", [256, NKEY], BF16)
        qiT, qiTb = io("in", f"qiT{L}", [512, NT], BF16)
        wtok, wtokb = io("in", f"wtok{L}", [9, 128, 8], F32)
        qT, qTb = io("in", f"qT{L}", [D, NT], BF16)
        cbias, cbiasb = io("c", "cbias", [128, 1024], F32)
        mtm_c, mtm_cb = io("c", "mtm_c", [NM, NM], BF16)
        pow2, pow2b = io("c", "pow2", [128, NBIS + 1], F32)
        wuk, wukb = io("w", f"w_uk{L}", [16, 128, 2, 128], F32)
        wuv, wuvb = io("w", f"w_uv{L}", [16, 128, 2, 128], F32)
        oT, oTb = io("out", f"oT{L}", [D, NT], BF16)
        ones, onesb = self.ones
        ident, identb = self.ident

        with ExitStack() as c0_:
            MT = [self.sb(f"b_mt{j}", [128, 8, 128 * (8 - j)], BF16, c0_) for j in range(8)]
            MTm, MTmb = self.sb("b_mtm", [NM, NT], BF16, c0_)
            if ntiles < 8:
                for j in range(8):
                    S.op("pool", "memset", [], [MT[j][1]], ap=MT[j][0][:], constant=1.0)
                S.op("pool", "memset", [], [MTmb], ap=MTm[:], constant=0.0)
            with ExitStack() as c:
                kin2, kin2b = self.sb("b_kin2", [128, NKEY], BF16, c)
                qi, qib = self.sb("b_qi", [128, 4, NT], BF16, c)
                wt, wtb = self.sb("b_wt", [128, 9, 8], F32, c)
                cb, cbb = self.sb("b_cb", [128, 1024], F32, c)
                p2, p2b = self.sb("b_p2", [128, NBIS + 1], F32, c)
                dg, dgb = self.sb("b_dg", [128, 8, 128], BF16, c)
                score, scoreb = self.sb("b_score", [128, NKEY], F32, c)
                mrow, mrowb = self.sb("b_mrow", [128, NKEY], BF16, c)
                rb = [self.sb(f"b_r{i}", [128, 512], BF16, c) for i in range(4)]
                sm, smb = self.sb("b_sm", [128, 8], F32, c)
                wk, wkb = self.sb("b_wk", [128, NBIS + 1], F32, c)

                S.op("dve", "memset", [], [smb], ap=sm[:], constant=0.0)
                S.dma("sp", None, [kinAb], [kin2b], out=kin2[0:64, :], in_=kinA)
                S.dma("sp", None, [kinAb], [kin2b], out=kin2[64:128, :], in_=kinA)
                S.dma("sp", None, [qiTb], [qib], out=qi[:], in_=qiT.rearrange("(ch p) n -> p ch n", p=128))
                S.dma("sp", None, [wtokb], [wtb], out=wt[:], in_=wtok.rearrange("t p h -> p t h"))
                S.dma("sp", None, [cbiasb], [cbb], out=cb[:], in_=cbias)
                S.dma("sp", None, [pow2b], [p2b], out=p2[:], in_=pow2)
                S.dma("sp", None, [mtm_cb], [MTmb], out=MTm[:, NX:NT], in_=mtm_c)
                ptaps = [self.ps[6][:, :].bitcast(BF16), self.ps[7][:, :].bitcast(BF16)]
                ptbs = [self.psb[6], self.psb[7]]
                acc_i = 0
                for j in range(ntiles):
                    L_ = 1024 * (j + 1) + NM
                    q0 = j * 128
                    for h in range(8):
                        S.op("dve", "tensor_scalar", [identb, wtb], [dgb], out=dg[:, h, :], in0=ident[:],
                             scalar1=wt[:, j, h:h + 1], scalar2=None, op0=ALU.mult)
                    chunks = [(s0, 512) for s0 in range(0, 1024 * (j + 1), 512)] + [(8192, NM)]
                    for (s0, n) in chunks:
                        d0 = s0 if s0 < 8192 else 1024 * (j + 1)
                        pa, pab = self.ps[4 + acc_i % 2], self.psb[4 + acc_i % 2]
                        acc_i += 1
                        for hp in range(4):
                            for half in range(2):
                                h = 2 * hp + half
                                lp, lpb = self.ps[(hp % 2) * 2 + half], self.psb[(hp % 2) * 2 + half]
                                pr = slice(64 * half, 64 * half + 64)
                                S.op("pe", "matmul", [qib, kin2b], [lpb], out=lp[:, :n],
                                     lhsT=qi[pr, hp, q0:q0 + 128], rhs=kin2[pr, s0:s0 + n], start=True, stop=True)
                                r, rbb = rb[(hp % 2) * 2 + half]
                                if half == 0:
                                    S.op("act", "activation", [lpb], [rbb], out=r[:, :n], in_=lp[:, :n], func=AF.Relu)
                                else:
                                    S.op("dve", "tensor_scalar", [lpb], [rbb], out=r[:, :n], in0=lp[:, :n],
                                         scalar1=0.0, scalar2=None, op0=ALU.max)
                            for half in range(2):
                                h = 2 * hp + half
                                r, rbb = rb[(hp % 2) * 2 + half]
                                S.op("pe", "matmul", [dgb, rbb], [pab], out=pa[:, :n], lhsT=dg[:, h, :], rhs=r[:, :n],
                                     start=(h == 0), stop=(h == 7))
                        self.evac(score[:, d0:d0 + n], pa[:, :n], [pab], [scoreb], eng="act")
                    S.op("dve", "tensor_reduce", [scoreb], [smb], out=sm[:, 0:1], in_=score[:, :L_], axis=AX.X,
                         op=ALU.max, apply_absolute_value=True)
                    S.op("dve", "tensor_scalar", [smb], [smb], out=sm[:, 1:2], in0=sm[:, 0:1], scalar1=1.0,
                         scalar2=None, op0=ALU.add)
                    S.op("dve", "tensor_scalar", [p2b, smb], [wkb], out=wk[:], in0=p2[:], scalar1=sm[:, 1:2],
                         scalar2=None, op0=ALU.mult)
                    S.op("dve", "memset", [], [smb], ap=sm[:, 2:3], constant=0.0)
                    S.op("pool", "tensor_tensor", [scoreb, cbb], [scoreb], out=score[:, 1024 * j:1024 * (j + 1)],
                         in0=score[:, 1024 * j:1024 * (j + 1)], in1=cb[:], op=ALU.add)
                    for k in range(NBIS):
                        S.op("dve", "tensor_scalar", [scoreb, smb], [mrowb, smb], out=mrow[:, :L_], in0=score[:, :L_],
                             scalar1=sm[:, 2:3], scalar2=None, op0=ALU.is_ge, op1=ALU.add, accum_out=sm[:, 3:4])
                        S.op("dve", "scalar_tensor_tensor", [smb, wkb], [smb], out=sm[:, 4:5], in0=sm[:, 3:4],
                             scalar=float(TOPK) - 0.5, in1=wk[:, k:k + 1], op0=ALU.is_ge, op1=ALU.mult)
                        S.op("dve", "scalar_tensor_tensor", [smb, wkb], [smb], out=sm[:, 2:3], in0=sm[:, 4:5],
                             scalar=wk[:, k + 1:k + 2], in1=sm[:, 2:3], op0=ALU.subtract, op1=ALU.add)
                    S.op("dve", "tensor_tensor", [smb, wkb], [smb], out=sm[:, 5:6], in0=sm[:, 2:3],
                         in1=wk[:, NBIS:NBIS + 1], op=ALU.subtract)
                    S.op("dve", "tensor_scalar", [scoreb, smb], [mrowb], out=mrow[:, :L_], in0=score[:, :L_],
                         scalar1=sm[:, 5:6], scalar2=None, op0=ALU.is_ge)
                    if dbg is not None and dbg == j:
                        d1, d1b = io("out", "dbg_score", [128, NKEY], F32)
                        d2, d2b = io("out", "dbg_sm", [128, 8], F32)
                        d3, d3b = io("out", "dbg_mrow", [128, NKEY], BF16)
                        S.dma("sp", None, [scoreb], [d1b], out=d1[:, :L_], in_=score[:, :L_])
                        S.dma("sp", None, [smb], [d2b], out=d2, in_=sm[:])
                        S.dma("sp", None, [mrowb], [d3b], out=d3[:, :L_], in_=mrow[:, :L_])
                    hb_i = 0
                    for jp in range(j + 1):
                        mt, mtb = MT[jp]
                        for r0 in range(0, 8, 4):
                            off = 0
                            ptb = ptbs[hb_i % 2]
                            ptb_ap = ptaps[hb_i % 2]
                            hb_i += 1
                            for r in range(4):
                                g = 8 * jp + r0 + r
                                S.op("pe", "transpose", [mrowb, identb], [ptb],
                                     out=ptb_ap[:, off + r * 128:off + (r + 1) * 128],
                                     in_=mrow[:, g * 128:(g + 1) * 128], identity=ident[:])
                            self.evac(mt[:, r0:r0 + 4, (j - jp) * 128:(j - jp + 1) * 128],
                                      ptb_ap[:, off:off + 512].rearrange("p (r t) -> p r t", t=128), [ptb], [mtb])
                    off = 0
                    ptb = ptbs[hb_i % 2]
                    ptb_ap = ptaps[hb_i % 2]
                    S.op("pe", "transpose", [mrowb, identb], [ptb], out=ptb_ap[:NM, off:off + 128],
                         in_=mrow[:, 1024 * (j + 1):1024 * (j + 1) + NM], identity=ident[:])
                    self.evac(MTm[:, q0:q0 + 128], ptb_ap[:NM, off:off + 128], [ptb], [MTmb])
            S.barrier()
            with ExitStack() as c:
                ckv, ckvb = self.sb("b_ckv", [128, 2, NKEY], BF16, c)
                Kh, Khb = self.sb("b_K", [128, NKEY], BF16, c)
                Vh, Vhb = self.sb("b_V", [128, 65, 128], BF16, c)
                qh = [self.sb(f"b_qh{i}", [128, NT], BF16, c) for i in range(2)]
                wk_s = [self.sb(f"b_wuk{i}", [128, 2, 128], BF16, c) for i in range(2)]
                wv_s = [self.sb(f"b_wuv{i}", [128, 2, 128], BF16, c) for i in range(2)]
                pe_t = [self.sb(f"b_pe{i}", [128, 512], BF16, c) for i in range(3)]
                pm_t = [self.sb(f"b_pm{i}", [128, 512], BF16, c) for i in range(4)]
                rec, recb = self.sb("b_rec", [128, NT], F32, c)
                osb = [self.sb(f"b_o{i}", [128, NT], BF16, c) for i in range(2)]
                ckv_v = ckvA.rearrange("(cc p) n -> p cc n", p=128)
                for s0 in range(0, NKEY, 2052):
                    S.dma("sp", None, [ckvAb], [ckvb], out=ckv[:, :, s0:s0 + 2052], in_=ckv_v[:, :, s0:s0 + 2052])
                SCL = 128.0 ** -0.5
                items = []
                items.append((64, NM, 0, [(0, 0, 512), (1, 512, 1024), (4, 1024, NT)]))
                for g in range(64):
                    jp = g // 8
                    st_ = 128 * jp
                    pcs = []
                    if st_ < 512:
                        pcs.append((0, st_, 512))
                        pcs.append((1, 512, 1024))
                    else:
                        pcs.append((1, st_, 1024))
                    items.append((g, 128, jp, pcs))
                last_touch = {}
                for ii, (g, npart, jp, pcs) in enumerate(items):
                    for (bk, c0, c1) in pcs:
                        last_touch[bk] = ii
                for h in range(nheads):
                    q_, qb_ = qh[h % 2]
                    wk_, wkb_ = wk_s[h % 2]
                    wv_, wvb_ = wv_s[h % 2]
                    S.dma("sp", None, [qTb], [qb_], out=q_[:], in_=qT[h * 128:(h + 1) * 128, :])
                    S.dma("pool", None, [wukb], [wkb_], out=wk_[:], in_=wuk[h])
                    S.dma("pool", None, [wuvb], [wvb_], out=wv_[:], in_=wuv[h])
                    kchunks = [(s0, 512) for s0 in range(0, 8192, 512)] + [(8192, NM)]
                    for ci, (s0, n) in enumerate(kchunks):
                        p, pb = self.ps[5 + ci % 2], self.psb[5 + ci % 2]
                        for cc in range(2):
                            S.op("pe", "matmul", [wkb_, ckvb], [pb], out=p[:, :n], lhsT=wk_[:, cc, :],
                                 rhs=ckv[:, cc, s0:s0 + n], start=(cc == 0), stop=(cc == 1))
                        self.evac(Kh[:, s0:s0 + n], p[:, :n], [pb], [Khb])
                    for g0 in range(0, 65, 4):
                        ng = min(4, 65 - g0)
                        p, pb = self.ps[5 + (g0 // 4) % 2], self.psb[5 + (g0 // 4) % 2]
                        for gg in range(ng):
                            g = g0 + gg
                            np_ = 128 if g < 64 else NM
                            for cc in range(2):
                                S.op("pe", "matmul", [ckvb, wvb_], [pb], out=p[:np_, gg * 128:(gg + 1) * 128],
                                     lhsT=ckv[:, cc, g * 128:g * 128 + np_], rhs=wv_[:, cc, :],
                                     start=(cc == 0), stop=(cc == 1))
                        if ng == 4:
                            self.evac(Vh[:, g0:g0 + 4, :], p[:, :].rearrange("p (g e) -> p g e", e=128), [pb], [Vhb])
                        else:
                            self.evac(Vh[:NM, 64, :], p[:NM, 0:128], [pb], [Vhb])
                    work = []
                    for ii, (g, npart, jp, pcs) in enumerate(items):
                        for (bk, c0, c1) in pcs:
                            work.append((ii, g, npart, jp, bk, c0, c1))
                    pend = []
                    for wi_, (ii, g, npart, jp, bk, c0, c1) in enumerate(work):
                        n = c1 - c0
                        sp_, spb = self.ps[5 + wi_ % 2], self.psb[5 + wi_ % 2]
                        kc0 = g * 128 if g < 64 else 8192
                        S.op("pe", "matmul", [Khb, qb_], [spb], out=sp_[:npart, :n], lhsT=Kh[:, kc0:kc0 + npart],
                             rhs=q_[:, c0:c1], start=True, stop=True)
                        pe_, peb = pe_t[wi_ % 3]
                        S.op("act", "activation", [spb], [peb], out=pe_[:npart, :n], in_=sp_[:npart, :n],
                             func=AF.Exp, scale=SCL)
                        pm_, pmb = pm_t[wi_ % 4]
                        if g < 64:
                            mt, mtb = MT[jp]
                            msk = mt[:, g % 8, c0 - 128 * jp:c1 - 128 * jp]
                        else:
                            mt, mtb = MTm, MTmb
                            msk = MTm[:, c0:c1]
                        S.op("dve", "tensor_tensor", [peb, mtb], [pmb], out=pm_[:npart, :n], in0=pe_[:npart, :n],
                             in1=msk, op=ALU.mult)
                        pend.append((ii, g, npart, bk, c0, c1, pm_, pmb))
                        if len(pend) > 2:
                            self._pv(pend.pop(0), Vh, Vhb, last_touch)
                    while pend:
                        self._pv(pend.pop(0), Vh, Vhb, last_touch)
                    o_, ob_ = osb[h % 2]
                    for (bk, dk, pc0, c0, c1) in [(0, 2, 0, 0, 512), (1, 3, 0, 512, 1024), (4, 4, 0, 1024, NT)]:
                        n = c1 - c0
                        if bk == 4:
                            den_ap = self.ps[4][:, NM:2 * NM]
                            o_ap = self.ps[4][:, 0:NM]
                        else:
                            den_ap = self.ps[dk][:, :n]
                            o_ap = self.ps[bk][:, :n]
                        S.op("dve", "reciprocal", [self.psb[dk]], [recb], out=rec[:, c0:c1], in_=den_ap)
                        S.op("dve", "tensor_tensor", [self.psb[bk], recb], [ob_], out=o_[:, c0:c1], in0=o_ap,
                             in1=rec[:, c0:c1], op=ALU.mult)
                    S.dma("sp", None, [ob_], [oTb], out=oT[h * 128:(h + 1) * 128, :], in_=o_[:])
            S.barrier()

    def phase_c(self, L, io, last=False):
        nc, S = self.nc, self.S
        xT, xTb = io("in", f"xT{L}", [D, NT], F32)
        hT, hTb = io("in", f"hT{L}", [D, NT], BF16)
        oT, oTb = io("in", f"oT{L}", [D, NT], BF16)
        pinT, pinTb = io("in", f"pinT{L}", [1024, NT], F32)
        halo, halob = io("in", f"halo{L}", [1024, 9, NM], F32)
        wblk, wblkb = io("w", f"w_in_blk{L}", [62, 128, KC, 128], F32)
        wao, waob = io("w", f"w_ao_blk{L}", [16, 128, KC, 128], F32)
        wpl, wplb = io("w", f"w_pool{L}", [4, 2, 128, 2, 128], F32)
        psc, pscb = io("w", f"pool_sc{L}", [128, 8], F32)
        wpo, wpob = io("w", f"w_po_blk{L}", [16, 128, 8, 128], F32)
        wou, woub = io("w", f"w_out_blk{L}", [16, 128, KC, 128], F32)
        gml, gmlb = io("w", f"g_mlp{L}", [128, KC], F32)
        w1, w1b = io("w", f"w1_blk{L}", [64, 128, KC, 128], F32)
        w2, w2b = io("w", f"w2_blk{L}", [16, 128, 64, 128], F32)
        invc, invcb = io("c", "invc", [128, 4, NM], F32)
        xN, xNb = io("out", f"xT{L + 1}", [D, NT], F32)
        if last:
            gfin, gfinb = io("w", "g_fin", [128, KC], F32)
            yT, yTb = io("out", "yT", [D, NT], F32)

        with ExitStack() as c_out:
            merged, mergedb = self.sb("c_merged", [128, KC, NT], BF16, c_out)
            yb, ybb = self.sb("c_yb", [128, 8, NT], BF16, c_out)
            with ExitStack() as c:
                F_ = 9 * 144
                P0, P0b = self.sb("c_p0", [128, 8, 9, 144], F32, c)
                S2, S2b = self.sb("c_s2", [128, 8, F_], F32, c)
                S4, S4b = self.sb("c_s4", [128, 6, F_], F32, c)
                S8, S8b = self.sb("c_s8", [128, 4, F_], F32, c)
                S16, S16b = self.sb("c_s16", [128, 2, F_], F32, c)
                ic, icb = self.sb("c_ic", [128, 4, NM], F32, c)
                tm, tmb = self.sb("c_tm", [128, NM], F32, c)
                S.op("pool", "memset", [], [P0b], ap=P0[:], constant=0.0)
                S.op("pool", "memset", [], [S2b], ap=S2[:, :, 0:8], constant=0.0)
                S.op("pool", "memset", [], [S4b], ap=S4[:, :, 0:8], constant=0.0)
                S.op("pool", "memset", [], [S8b], ap=S8[:, :, 0:8], constant=0.0)
                S.dma("sp", None, [invcb], [icb], out=ic[:], in_=invc)
                for ch in range(8):
                    S.dma("sp", None, [pinTb], [P0b], out=P0[:, ch, 0:8, NM:144],
                          in_=pinT[ch * 128:(ch + 1) * 128, 0:NX].rearrange("p (s t) -> p s t", t=128))
                    S.dma("sp", None, [pinTb], [P0b], out=P0[:, ch, 8, NM:2 * NM], in_=pinT[ch * 128:(ch + 1) * 128, NX:NT])
                    S.dma("sp", None, [halob], [P0b], out=P0[:, ch, :, 0:NM], in_=halo[ch * 128:(ch + 1) * 128, :, :])
                Pf = P0[:].rearrange("p c s u -> p c (s u)")
                S.op("pool", "tensor_tensor", [P0b], [S2b], out=S2[:, :, 1:F_], in0=Pf[:, :, 1:F_], in1=Pf[:, :, 0:F_ - 1], op=ALU.add)
                S.op("pool", "tensor_tensor", [S2b], [S4b], out=S4[:, :, 2:F_], in0=S2[:, 2:8, 2:F_], in1=S2[:, 2:8, 0:F_ - 2], op=ALU.add)
                S.op("pool", "tensor_tensor", [S4b], [S8b], out=S8[:, :, 4:F_], in0=S4[:, 2:6, 4:F_], in1=S4[:, 2:6, 0:F_ - 4], op=ALU.add)
                S.op("pool", "tensor_tensor", [S8b], [S16b], out=S16[:, :, 8:F_], in0=S8[:, 2:4, 8:F_], in1=S8[:, 2:4, 0:F_ - 8], op=ALU.add)
                for ch in range(8):
                    g = ch // 2
                    w = [2, 4, 8, 16][g]
                    src, srcb, sch = [(S2, S2b, ch), (S4, S4b, ch - 2), (S8, S8b, ch - 4), (S16, S16b, ch - 6)][g]
                    sv = src[:, sch, :].rearrange("p (s u) -> p s u", u=144)
                    S.op("dve", "scalar_tensor_tensor", [srcb, P0b], [ybb],
                         out=yb[:, ch, 0:NX].rearrange("p (s t) -> p s t", t=128), in0=sv[:, 0:8, NM:144],
                         scalar=1.0 / w, in1=P0[:, ch, 0:8, NM:144], op0=ALU.mult, op1=ALU.subtract)
                    S.op("dve", "tensor_tensor", [srcb, icb], [tmb], out=tm[:], in0=sv[:, 8, NM:2 * NM], in1=ic[:, g, :], op=ALU.mult)
                    S.op("dve", "tensor_tensor", [tmb, P0b], [ybb], out=yb[:, ch, NX:NT], in0=tm[:], in1=P0[:, ch, 8, NM:2 * NM], op=ALU.subtract)
            S.barrier()
            with ExitStack() as c:
                hb, hbb = self.sb("c_hb", [128, KC, NT], BF16, c)
                ob, obb = self.sb("c_ob", [128, KC, NT], BF16, c)
                b1, b1b = self.sb("c_b1", [128, 8, NT], BF16, c)
                ps_, psb_ = self.sb("c_psc", [128, 8], F32, c)
                slots = [self.sb(f"c_ws{i}", [128, KC, 128], BF16, c) for i in range(3)]
                tA = [self.sb(f"c_tA{i}", [128, NT], F32, c) for i in range(2)]
                tB = [self.sb(f"c_tB{i}", [128, NT], F32, c) for i in range(2)]
                S.dma("sp", None, [pscb], [psb_], out=ps_[:], in_=psc)
                hv = hT.rearrange("(kc p) n -> p kc n", p=128)
                ov = oT.rearrange("(kc p) n -> p kc n", p=128)
                for k0 in range(0, KC, 4):
                    S.dma("sp", None, [hTb], [hbb], out=hb[:, k0:k0 + 4, :], in_=hv[:, k0:k0 + 4, :])
                    S.dma("sp", None, [oTb], [obb], out=ob[:, k0:k0 + 4, :], in_=ov[:, k0:k0 + 4, :])
                blocks = []
                for g in range(4):
                    for db in range(2):
                        blocks.append((wpl[g, db], wplb, 128, yb[:, 2 * g:2 * g + 2, :], ybb, 2))

                def epi_pool(bi, banks):
                    ch = bi
                    for (p, pb, c0, c1) in banks:
                        S.op("dve", "tensor_scalar", [pb, psb_], [b1b], out=b1[:, ch, c0:c1], in0=p[:, :c1 - c0],
                             scalar1=ps_[:, ch:ch + 1], scalar2=None, op0=ALU.mult)

                self.gemm(blocks, epi_pool, slots, [[0, 1, 2], [3, 4, 5]])
                blocks = []
                for n in range(16):
                    blocks.append((wblk[30 + n], wblkb, 128, hb, hbb, KC))
                    blocks.append((wao[n], waob, 128, ob, obb, KC))
                    blocks.append((wblk[46 + n], wblkb, 128, hb, hbb, KC))
                    blocks.append((wpo[n], wpob, 128, b1, b1b, 8))

                def epi_merge(bi, banks):
                    n, kind = bi // 4, bi % 4
                    sg, sgb = tA[(bi // 2) % 2]
                    m1, m1b = tB[0]
                    t2, t2b = tB[1]
                    for (p, pb, c0, c1) in banks:
                        if kind in (0, 2):
                            S.op("act", "activation", [pb], [sgb], out=sg[:, c0:c1], in_=p[:, :c1 - c0], func=AF.Sigmoid)
                        elif kind == 1:
                            S.op("dve", "tensor_tensor", [pb, sgb], [m1b], out=m1[:, c0:c1], in0=p[:, :c1 - c0],
                                 in1=sg[:, c0:c1], op=ALU.mult)
                        else:
                            S.op("dve", "tensor_tensor", [pb, sgb], [t2b], out=t2[:, c0:c1], in0=p[:, :c1 - c0],
                                 in1=sg[:, c0:c1], op=ALU.mult)
                    if kind == 3:
                        S.op("pool", "tensor_tensor", [m1b, t2b], [mergedb], out=merged[:, n, :], in0=m1[:], in1=t2[:], op=ALU.add)

                self.gemm(blocks, epi_merge, slots, [[0, 1, 2], [3, 4, 5]])
            S.barrier()
            with ExitStack() as c:
                x2, x2b = self.sb("c_x2", [128, KC, NT], F32, c)
                h2, h2b = self.sb("c_h2", [128, KC, NT], BF16, c)
                u, ub = self.sb("c_u", [128, 8, NT], BF16, c)
                slots = [self.sb(f"c_wt{i}", [128, KC, 128], BF16, c) for i in range(3)]
                xs = [self.sb(f"c_xs{i}", [128, NT], F32, c) for i in range(2)]
                rstd, rstdb = self.sb("c_rstd", [128, NT], F32, c)
                rt = [self.sb(f"c_rt{i}", [128, 352], F32, c) for i in range(3)]
                g1, g1b = self.sb("c_g1", [128, KC], F32, c)
                S.dma("sp", None, [gmlb], [g1b], out=g1[:], in_=gml)
                blocks = [(wou[n], woub, 128, merged, mergedb, KC) for n in range(16)]

                def epi_out(bi, banks):
                    n = bi
                    xs_, xsb_ = xs[n % 2]
                    S.dma("sp", None, [xTb], [xsb_], out=xs_[:], in_=xT[n * 128:(n + 1) * 128, :])
                    for (p, pb, c0, c1) in banks:
                        S.op("dve", "tensor_tensor", [pb, xsb_], [x2b], out=x2[:, n, c0:c1], in0=p[:, :c1 - c0],
                             in1=xs_[:, c0:c1], op=ALU.add)

                self.gemm(blocks, epi_out, slots, [[0, 1, 2], [3, 4, 5]])
                self.rms_rstd(x2, x2b, KC, 128, D, h2, h2b, rstd, rstdb, [0, 1, 2])
                for kc in range(KC):
                    S.op("dve", "scalar_tensor_tensor", [x2b, g1b, rstdb], [h2b], out=h2[:, kc, :], in0=x2[:, kc, :],
                         scalar=g1[:, kc:kc + 1], in1=rstd[:], op0=ALU.mult, op1=ALU.mult)
                blocks = []
                for f in range(8):
                    for m in range(8):
                        blocks.append((w1[8 * f + m], w1b, 128, h2, h2b, KC))
                    for n in range(16):
                        blocks.append((w2[n, :, 8 * f:8 * f + 8, :], w2b, 128, u, ub, 8))
                cnt = {"r": 0}

                def epi_mlp(bi, banks):
                    f, r = bi // 24, bi % 24
                    if r < 8:
                        m = r
                        for (p, pb, c0, c1) in banks:
                            rt_, rtb_ = rt[cnt["r"] % 3]
                            cnt["r"] += 1
                            S.op("act", "activation", [pb], [rtb_], out=rt_[:, :c1 - c0], in_=p[:, :c1 - c0], func=AF.Relu)
                            S.op("dve", "scalar_tensor_tensor", [pb, rtb_], [ub], out=u[:, m, c0:c1], in0=p[:, :c1 - c0],
                                 scalar=0.0, in1=rt_[:, :c1 - c0], op0=ALU.max, op1=ALU.mult)
                    else:
                        n = r - 8
                        for (p, pb, c0, c1) in banks:
                            S.op("dve", "tensor_tensor", [pb, x2b], [x2b], out=x2[:, n, c0:c1], in0=p[:, :c1 - c0],
                                 in1=x2[:, n, c0:c1], op=ALU.add)

                self.gemm(blocks, epi_mlp, slots, [[0, 1, 2], [3, 4, 5]])
                xv = xN.rearrange("(kc p) n -> p kc n", p=128)
                for k0 in range(0, KC, 4):
                    S.dma("sp", None, [x2b], [xNb], out=xv[:, k0:k0 + 4, :], in_=x2[:, k0:k0 + 4, :])
                if last:
                    S.dma("sp", None, [gfinb], [g1b], out=g1[:], in_=gfin)
                    self.rms_rstd(x2, x2b, KC, 128, D, h2, h2b, rstd, rstdb, [0, 1, 2])
                    yv = yT.rearrange("(kc p) n -> p kc n", p=128)
                    for kc in range(KC):
                        xs_, xsb_ = xs[kc % 2]
                        S.op("dve", "scalar_tensor_tensor", [x2b, g1b, rstdb], [xsb_], out=xs_[:], in0=x2[:, kc, :],
                             scalar=g1[:, kc:kc + 1], in1=rstd[:], op0=ALU.mult, op1=ALU.mult)
                        S.dma("sp", None, [xsb_], [yTb], out=yv[:, kc, :], in_=xs_[:])
            S.barrier()

    def _pv(self, item, Vh, Vhb, last_touch):
        S = self.S
        ones, onesb = self.ones
        ii, g, npart, bk, c0, c1, pm_, pmb = item
        n = c1 - c0
        first = (g == 64)
        last = (last_touch[bk] == ii)
        if bk == 4:
            o_ap = self.ps[4][:, 0:NM]
            d_ap = self.ps[4][:, NM:2 * NM]
            ob, db = self.psb[4], self.psb[4]
        else:
            pc0 = c0 - 512 * bk
            o_ap = self.ps[bk][:, pc0:pc0 + n]
            d_ap = self.ps[2 + bk][:, pc0:pc0 + n]
            ob, db = self.psb[bk], self.psb[2 + bk]
        S.op("pe", "matmul", [Vhb, pmb], [ob], out=o_ap, lhsT=Vh[:npart, g, :], rhs=pm_[:npart, :n],
             start=first, stop=last, skip_group_check=(bk == 4))
        S.op("pe", "matmul", [onesb, pmb], [db], out=d_ap, lhsT=ones[:npart, :], rhs=pm_[:npart, :n],
             start=first, stop=last, skip_group_check=(bk == 4))

    def finish(self):
        S = self.S
        S.barrier()
        S.emit()
        self.ctx.close()
        return self.nc


def blk(W):
    K, N = W.shape
    return np.ascontiguousarray(W.reshape(K // 128, 128, N // 128, 128).transpose(2, 1, 0, 3))


def win_layout(w_in_l):
    cols = np.concatenate([np.arange(c0, c0 + w) for (c0, w) in WIN_BLOCKS])
    wb = blk(w_in_l[:, cols])
    wki = np.ascontiguousarray(w_in_l[:, KI_COL:KI_COL + 64].reshape(KC, 128, 64).transpose(1, 0, 2))
    wwi = np.ascontiguousarray(w_in_l[:, WI_COL:WI_COL + 8].reshape(KC, 128, 8).transpose(1, 0, 2))
    return wb, wki, wwi


def vec_layout(g):
    return np.ascontiguousarray(g.reshape(-1, 128).T)


def x_layout(x, meta, c):
    xs = x.reshape(8, 8, 128, D)[:, c].reshape(NX, D)
    full = np.concatenate([xs, meta], axis=0)
    return np.ascontiguousarray(full.T)


def gather_keys(per_core):
    F = per_core[0].shape[0]
    out = np.empty((F, NKEY), per_core[0].dtype)
    xs = out[:, :8192].reshape(F, 8, 8, 128)
    for c in range(8):
        xs[:, :, c, :] = per_core[c][:, :NX].reshape(F, 8, 128)
    out[:, 8192:] = per_core[0][:, NX:]
    return out


def uk_layout(w):
    return np.ascontiguousarray(w.reshape(16, 2, 128, 128).transpose(0, 2, 1, 3))


def consts(c):
    import ml_dtypes
    i = np.arange(128)[:, None]
    col = np.arange(1024)[None, :]
    r = col // 128
    ip = col % 128
    vis = (r < c) | ((r == c) & (ip <= i))
    cbias = np.where(vis, 0.0, NEG).astype(np.float32)
    kk = np.arange(NM)[:, None]
    qq = np.arange(NM)[None, :]
    mtm = (kk <= qq).astype(np.float32).astype(ml_dtypes.bfloat16)
    pow2 = np.tile((2.0 ** -np.arange(NBIS + 1)).astype(np.float32)[None, :], (128, 1))
    return {"cbias": cbias, "mtm_c": mtm, "pow2": pow2}


def pool_layout(wp):
    return np.ascontiguousarray(wp.reshape(4, 2, 128, 2, 128).transpose(0, 3, 2, 1, 4))


def invc_const():
    out = np.zeros((128, 4, NM), np.float32)
    for g, w in enumerate([2, 4, 8, 16]):
        out[:, g, :] = 1.0 / np.minimum(np.arange(NM) + 1, w)
    return out


def halo_build(pin_cores):
    outs = []
    for c in range(8):
        h = np.zeros((1024, 9, NM), np.float32)
        for j in range(8):
            if c > 0:
                src = pin_cores[c - 1][:, 128 * j + 112:128 * j + 128]
            elif j > 0:
                src = pin_cores[7][:, 128 * (j - 1) + 112:128 * (j - 1) + 128]
            else:
                src = pin_cores[0][:, NX:NT]
            h[:, j, :] = src
        outs.append(h)
    return outs


_PROGS = {}


def _build(which):
    if which in _PROGS:
        return _PROGS[which]
    kb = KB()

    def io(kind, name, shape, dtype):
        return kb.dram(name, shape, dtype, "out" if kind == "out" else "in")

    if which == "A":
        kb.phase_a(0, io)
    elif which == "B":
        kb.phase_b(0, io)
    elif which == "C":
        kb.phase_c(0, io, last=False)
    elif which == "Clast":
        kb.phase_c(0, io, last=True)
    nc = kb.finish()
    _PROGS[which] = nc
    return nc


def _run(which, in_maps):
    nc = _build(which)
    res = run_bass_kernel_spmd(nc, in_maps, core_ids=list(range(NCORE)))
    return res.results


def layer_weights(inp, l):
    wb, wki, wwi = win_layout(inp["w_in"][l])
    W = {
        "w_in_blk0": wb, "w_ki0": wki, "w_wi0": wwi,
        "g_mix0": vec_layout(inp["norm_mix_g"][l]),
        "g_kv0": vec_layout(inp["kv_norm_g"][l]),
        "g_ki0": np.ascontiguousarray(inp["idx_k_norm_g"][l].reshape(64, 1)),
        "w_uk0": uk_layout(inp["w_uk"][l]), "w_uv0": uk_layout(inp["w_uv"][l]),
        "w_ao_blk0": blk(inp["w_attn_o"][l]),
        "w_pool0": pool_layout(inp["w_pool"][l]),
        "pool_sc0": vec_layout(inp["pool_scale"][l]),
        "w_po_blk0": blk(inp["w_pool_o"][l]),
        "w_out_blk0": blk(inp["w_out"][l]),
        "g_mlp0": vec_layout(inp["norm_mlp_g"][l]),
        "w1_blk0": blk(inp["w_mlp_in"][l]),
        "w2_blk0": blk(inp["w_mlp_out"][l]),
    }
    return W


A_IN = ["w_in_blk0", "w_ki0", "w_wi0", "g_mix0", "g_kv0", "g_ki0"]
B_IN = ["w_uk0", "w_uv0"]
C_IN = ["w_in_blk0", "w_ao_blk0", "w_pool0", "pool_sc0", "w_po_blk0", "w_out_blk0", "g_mlp0", "w1_blk0", "w2_blk0"]


def kernel(x, meta_tokens, norm_mix_g, w_in, kv_norm_g, idx_k_norm_g, w_uk, w_uv, w_attn_o,
           w_pool, pool_scale, w_pool_o, w_out, norm_mlp_g, w_mlp_in, w_mlp_out, final_norm_g, _nlayers=DEPTH, _dbg=None):
    inp = dict(x=x, meta_tokens=meta_tokens, norm_mix_g=norm_mix_g, w_in=w_in, kv_norm_g=kv_norm_g,
               idx_k_norm_g=idx_k_norm_g, w_uk=w_uk, w_uv=w_uv, w_attn_o=w_attn_o, w_pool=w_pool,
               pool_scale=pool_scale, w_pool_o=w_pool_o, w_out=w_out, norm_mlp_g=norm_mlp_g,
               w_mlp_in=w_mlp_in, w_mlp_out=w_mlp_out, final_norm_g=final_norm_g)
    inp = {k: np.asarray(v, dtype=np.float32) for k, v in inp.items()}
    xs = [x_layout(inp["x"][0], inp["meta_tokens"], c) for c in range(NCORE)]
    cst = [consts(c) for c in range(NCORE)]
    invc = invc_const()
    gfin = vec_layout(inp["final_norm_g"])
    ys = None
    for l in range(_nlayers):
        W = layer_weights(inp, l)
        last = (l == _nlayers - 1)
        ra = _run("A", [dict({k: W[k] for k in A_IN}, xT0=xs[c]) for c in range(NCORE)])
        kinA = gather_keys([np.asarray(ra[c]["kinT0"]) for c in range(NCORE)])
        ckvA = gather_keys([np.asarray(ra[c]["ckvnT0"]) for c in range(NCORE)])
        halos = halo_build([np.asarray(ra[c]["pinT0"]) for c in range(NCORE)])
        rb = _run("B", [dict({k: W[k] for k in B_IN}, kinA0=kinA, ckvA0=ckvA, qiT0=np.asarray(ra[c]["qiT0"]),
                             wtok0=np.asarray(ra[c]["wtok0"]), qT0=np.asarray(ra[c]["qT0"]), **cst[c])
                        for c in range(NCORE)])
        cin = []
        for c in range(NCORE):
            m = dict({k: W[k] for k in C_IN}, xT0=xs[c], hT0=np.asarray(ra[c]["hT0"]), oT0=np.asarray(rb[c]["oT0"]),
                     pinT0=np.asarray(ra[c]["pinT0"]), halo0=halos[c], invc=invc)
            if last:
                m["g_fin"] = gfin
            cin.append(m)
        rc = _run("Clast" if last else "C", cin)
        xs = [np.asarray(rc[c]["xT1"]) for c in range(NCORE)]
        if _dbg is not None:
            _dbg.append(dict(ra=ra, rb=rb, rc=rc))
        if last:
            ys = [np.asarray(rc[c]["yT"]) for c in range(NCORE)]
    out = np.empty((8, 8, 128, D), np.float32)
    for c in range(NCORE):
        out[:, c] = ys[c][:, :NX].T.reshape(8, 128, D)
    return out.reshape(1, 8192, D)
```
